# Optimizing a Trainium2 kernel written in Bass

```python
import jax, jax.numpy as jnp
from jax import lax
import numpy as np

D_MODEL = 1024
BATCH = 2
SEQ = 16384
DEPTH = 2

CHUNK = 64
N_MEM = 256
HEAD_DIM = 64
FOX_HEADS = 8
CHUNK_HEADS = 8
FOX_WIDTH = FOX_HEADS * HEAD_DIM
CHUNK_WIDTH = CHUNK_HEADS * HEAD_DIM
IN_COLS = 3 * FOX_WIDTH + FOX_HEADS + 3 * CHUNK_WIDTH
Q_BLOCK = 128
LEFT_CHUNKS = 8
REL_CLIP = 128
NUM_REL = CHUNK + REL_CLIP
MEM_HEADS = 4
MEM_HEAD_DIM = D_MODEL // MEM_HEADS
D_FF = -(-8 * D_MODEL // 768) * 256
EPS = 1e-6
NEG_INF = -1e30

kernel_name = "hybrid_fox_chunkrel_memxattn_block"


def rms_norm(x, g):
    xf = x.astype(jnp.float32)
    y = xf * lax.rsqrt(jnp.mean(xf * xf, axis=-1, keepdims=True) + EPS)
    return (y * g.astype(jnp.float32)).astype(x.dtype)


def fox_attention(q, k, v, log_f):
    B, S, H, D = q.shape
    nb = S // Q_BLOCK
    scale = D ** -0.5
    c = lax.cumsum(log_f, axis=1)
    c_keys = jnp.transpose(c, (0, 2, 1))
    k_pos = jnp.arange(S)
    q_blocks = jnp.transpose(q.reshape(B, nb, Q_BLOCK, H, D), (1, 0, 2, 3, 4))
    c_blocks = jnp.transpose(c.reshape(B, nb, Q_BLOCK, H), (1, 0, 3, 2))

    def one_block(args):
        i, q_i, c_i = args
        q_pos = i * Q_BLOCK + jnp.arange(Q_BLOCK)
        s = jnp.einsum('bqhd,bkhd->bhqk', q_i, k).astype(jnp.float32) * scale
        s = s + c_i[..., :, None] - c_keys[:, :, None, :]
        s = jnp.where((k_pos[None, :] <= q_pos[:, None])[None, None], s, NEG_INF)
        p = jax.nn.softmax(s, axis=-1)
        return jnp.einsum('bhqk,bkhd->bqhd', p.astype(v.dtype), v)

    out = lax.map(one_block, (jnp.arange(nb), q_blocks, c_blocks))
    return jnp.transpose(out, (1, 0, 2, 3, 4)).reshape(B, S, H, D)


def chunked_rel_attention(q, k, v, rel_bias):
    B, S, H, D = q.shape
    nc = S // CHUNK
    band = (LEFT_CHUNKS + 1) * CHUNK
    pad = LEFT_CHUNKS * CHUNK
    scale = D ** -0.5
    kp = jnp.pad(k, ((0, 0), (pad, 0), (0, 0), (0, 0)))
    vp = jnp.pad(v, ((0, 0), (pad, 0), (0, 0), (0, 0)))
    idx = (jnp.arange(nc) * CHUNK)[:, None] + jnp.arange(band)[None, :]
    k_band = kp[:, idx]
    v_band = vp[:, idx]
    q_c = q.reshape(B, nc, CHUNK, H, D)
    s = jnp.einsum('bnqhd,bnkhd->bhnqk', q_c, k_band).astype(jnp.float32) * scale
    dist = jnp.arange(CHUNK)[:, None] + pad - jnp.arange(band)[None, :]
    rel_idx = jnp.clip(dist, -(CHUNK - 1), REL_CLIP) + (CHUNK - 1)
    bias = rel_bias.astype(jnp.float32)[:, rel_idx]
    s = s + bias[None, :, None]
    valid = (idx - pad) >= 0
    s = jnp.where(valid[None, None, :, None, :], s, NEG_INF)
    p = jax.nn.softmax(s, axis=-1)
    o = jnp.einsum('bhnqk,bnkhd->bnqhd', p.astype(v.dtype), v_band)
    return o.reshape(B, S, H, D)


def memory_cross_attention(h, m, w_q, w_kv, w_o):
    B, S, _ = h.shape
    q = (h @ w_q).reshape(B, S, MEM_HEADS, MEM_HEAD_DIM)
    k, v = jnp.split(m @ w_kv, 2, axis=-1)
    k = k.reshape(B, N_MEM, MEM_HEADS, MEM_HEAD_DIM)
    v = v.reshape(B, N_MEM, MEM_HEADS, MEM_HEAD_DIM)
    s = jnp.einsum('bshd,bmhd->bhsm', q, k).astype(jnp.float32) * (MEM_HEAD_DIM ** -0.5)
    p = jax.nn.softmax(s, axis=-1)
    o = jnp.einsum('bhsm,bmhd->bshd', p.astype(v.dtype), v).reshape(B, S, D_MODEL)
    return o @ w_o


def swiglu(h, w_gate_up, w_down):
    g, u = jnp.split(h @ w_gate_up, 2, axis=-1)
    return (jax.nn.silu(g) * u) @ w_down


def setup_inputs(seed: int = 0) -> dict:
    key = jax.random.key(seed)
    ks = jax.random.split(key, 24)

    def nrm(k, shape, scale):
        return scale * jax.random.normal(k, shape, jnp.float32)

    def gain(k, n):
        return 1.0 + nrm(k, (DEPTH, n), 0.05)

    s_d = D_MODEL ** -0.5
    w_in = jnp.concatenate([
        nrm(ks[0], (DEPTH, D_MODEL, 3 * FOX_WIDTH), s_d),
        nrm(ks[1], (DEPTH, D_MODEL, FOX_HEADS), 0.1 * s_d),
        nrm(ks[2], (DEPTH, D_MODEL, 3 * CHUNK_WIDTH), s_d),
    ], axis=-1)
    b_fgate = jnp.linspace(3.0, 7.0, FOX_HEADS)[None, :] + nrm(ks[3], (DEPTH, FOX_HEADS), 0.3)
    return {
        "x": jax.random.normal(ks[4], (BATCH, SEQ, D_MODEL), jnp.float32),
        "mem": jax.random.normal(ks[5], (BATCH, N_MEM, D_MODEL), jnp.float32),
        "g_mix_pre": gain(ks[6], D_MODEL),
        "w_in": w_in,
        "b_fgate": b_fgate,
        "rel_bias": nrm(ks[7], (DEPTH, CHUNK_HEADS, NUM_REL), 0.5),
        "g_fox_out": gain(ks[8], FOX_WIDTH),
        "g_chunk_out": gain(ks[9], CHUNK_WIDTH),
        "w_out": nrm(ks[10], (DEPTH, D_MODEL, D_MODEL), s_d),
        "g_mix_post": gain(ks[11], D_MODEL),
        "g_mem_pre": gain(ks[12], D_MODEL),
        "g_mem_kv": gain(ks[13], D_MODEL),
        "w_mem_q": nrm(ks[14], (DEPTH, D_MODEL, D_MODEL), s_d),
        "w_mem_kv": nrm(ks[15], (DEPTH, D_MODEL, 2 * D_MODEL), s_d),
        "w_mem_o": nrm(ks[16], (DEPTH, D_MODEL, D_MODEL), s_d),
        "g_mem_post": gain(ks[17], D_MODEL),
        "g_ffn_pre": gain(ks[18], D_MODEL),
        "w_gate_up": nrm(ks[19], (DEPTH, D_MODEL, 2 * D_FF), s_d),
        "w_down": nrm(ks[20], (DEPTH, D_FF, D_MODEL), D_FF ** -0.5),
        "g_ffn_post": gain(ks[21], D_MODEL),
    }


def reference(x, mem, g_mix_pre, w_in, b_fgate, rel_bias, g_fox_out, g_chunk_out, w_out,
              g_mix_post, g_mem_pre, g_mem_kv, w_mem_q, w_mem_kv, w_mem_o, g_mem_post,
              g_ffn_pre, w_gate_up, w_down, g_ffn_post):
    B, S, _ = x.shape
    splits = [FOX_WIDTH, 2 * FOX_WIDTH, 3 * FOX_WIDTH, 3 * FOX_WIDTH + FOX_HEADS,
              3 * FOX_WIDTH + FOX_HEADS + CHUNK_WIDTH, 3 * FOX_WIDTH + FOX_HEADS + 2 * CHUNK_WIDTH]
    for l in range(DEPTH):
        h = rms_norm(x, g_mix_pre[l])
        proj = h @ w_in[l]
        q_f, k_f, v_f, f_logit, q_c, k_c, v_c = jnp.split(proj, splits, axis=-1)
        log_f = jax.nn.log_sigmoid((f_logit + b_fgate[l]).astype(jnp.float32))
        o_f = fox_attention(q_f.reshape(B, S, FOX_HEADS, HEAD_DIM),
                            k_f.reshape(B, S, FOX_HEADS, HEAD_DIM),
                            v_f.reshape(B, S, FOX_HEADS, HEAD_DIM), log_f)
        o_c = chunked_rel_attention(q_c.reshape(B, S, CHUNK_HEADS, HEAD_DIM),
                                    k_c.reshape(B, S, CHUNK_HEADS, HEAD_DIM),
                                    v_c.reshape(B, S, CHUNK_HEADS, HEAD_DIM), rel_bias[l])
        o_f = rms_norm(o_f.reshape(B, S, FOX_WIDTH), g_fox_out[l])
        o_c = rms_norm(o_c.reshape(B, S, CHUNK_WIDTH), g_chunk_out[l])
        mix = jnp.concatenate([o_f, o_c], axis=-1) @ w_out[l]
        x = x + rms_norm(mix, g_mix_post[l])
        h = rms_norm(x, g_mem_pre[l])
        m = rms_norm(mem, g_mem_kv[l])
        ca = memory_cross_attention(h, m, w_mem_q[l], w_mem_kv[l], w_mem_o[l])
        x = x + rms_norm(ca, g_mem_post[l])
        h = rms_norm(x, g_ffn_pre[l])
        x = x + rms_norm(swiglu(h, w_gate_up[l], w_down[l]), g_ffn_post[l])
    return x
```

```python
import numpy as np
import ml_dtypes
import concourse.bass as bass
import concourse.mybir as mybir
from concourse.bass_utils import run_bass_kernel_spmd

F32 = mybir.dt.float32
BF16 = mybir.dt.bfloat16
AF = mybir.ActivationFunctionType
ALU = mybir.AluOpType

D = 1024
SEQ = 16384
NT = 4096
DEPTH = 2
DFF = 2816
NFC = DFF // 128
EPS = 1e-6
NEG = -1e30
WPIECES = dict(w_out=1, w_mem_q=1, w_mem_kv=2, w_mem_o=1, w_gate_up=4, w_down=2)
GROUPS = [[0, 1, 2, 3], [4, 5, 6, 7]]
SB_BASE = 16512
SB_END = 229376


class Res:
    __slots__ = ("name", "last_w", "readers")

    def __init__(self, name):
        self.name = name
        self.last_w = None
        self.readers = []


STRICT = ("act", "dve", "pool")


class Sched:
    def __init__(self, nc):
        self.nc = nc
        self.eng = {"pe": nc.tensor, "act": nc.scalar, "dve": nc.vector, "pool": nc.gpsimd, "sp": nc.sync}
        self.ops = []
        self.res = {}

    def R(self, name):
        r = self.res.get(name)
        if r is None:
            r = Res(name)
            self.res[name] = r
        return r

    def barrier(self):
        last = {}
        for i, o in enumerate(self.ops):
            if o["dma"] is not None and o["dma"].startswith("wg_"):
                continue
            last[("dma", o["dma"]) if o["dma"] is not None else ("eng", o["engine"])] = i
        deps = sorted(last.values())
        for e in self.eng:
            self.op(e, lambda E: E.nop(), extra_deps=deps)

    def op(self, engine, fn, reads=(), writes=(), dma=None, inc=16, extra_deps=()):
        idx = len(self.ops)
        deps = set(extra_deps)
        rr = [self.R(r) for r in reads]
        ww = [self.R(w) for w in writes]
        for r in rr:
            if r.last_w is not None:
                deps.add(r.last_w)
            if r.name.startswith("ps"):
                deps.update(x for x in r.readers if self.ops[x]["engine"] != engine)
        for w in ww:
            if w.last_w is not None:
                deps.add(w.last_w)
            deps.update(w.readers)
        for r in rr:
            if dma is None:
                r.readers = [x for x in r.readers if self.ops[x]["dma"] is not None or self.ops[x]["engine"] != engine]
            r.readers.append(idx)
        for w in ww:
            w.last_w = idx
            w.readers = []
        deps.discard(idx)
        self.ops.append(dict(engine=engine, fn=fn, deps=sorted(deps), dma=dma, inc=inc, sig=False))
        return idx

    def emit(self):
        nc = self.nc
        ops = self.ops
        for o in ops:
            for d in o["deps"]:
                od = ops[d]
                if od["dma"] is not None or od["engine"] != o["engine"] or o["dma"] is not None \
                        or o["engine"] in STRICT:
                    od["sig"] = True
        eng_sem = {e: nc.alloc_semaphore(f"sem_{e}") for e in self.eng}
        eng_cnt = {e: 0 for e in self.eng}
        dma_sems, dma_cnt = {}, {}
        for o in ops:
            if o["dma"] is not None:
                k = o["dma"]
                if k not in dma_sems:
                    dma_sems[k] = nc.alloc_semaphore(f"dsem_{k}")
                    dma_cnt[k] = 0
                dma_cnt[k] += o["inc"]
                o["sigval"] = (dma_sems[k], dma_cnt[k])
            elif o["sig"]:
                e = o["engine"]
                eng_cnt[e] += 1
                o["sigval"] = (eng_sem[e], eng_cnt[e])
        seen = {e: {} for e in self.eng}
        for o in ops:
            e = o["engine"]
            E = self.eng[e]
            need = {}
            for d in o["deps"]:
                od = ops[d]
                if not od["sig"]:
                    continue
                if od["dma"] is None and od["engine"] == e and o["dma"] is None and e not in STRICT:
                    continue
                sem, val = od["sigval"]
                if need.get(sem.num, (None, 0))[1] < val:
                    need[sem.num] = (sem, val)
            for num, (sem, val) in need.items():
                if seen[e].get(num, 0) >= val:
                    continue
                seen[e][num] = val
                E.wait_ge(sem, val)
            ins = o["fn"](E)
            if o["dma"] is not None:
                ins.then_inc(o["sigval"][0], o["inc"])
            elif o["sig"]:
                ins.then_inc(o["sigval"][0], 1)
        self.counts = (eng_cnt, dma_cnt)


class Builder:
    B_GROUPS = 32
    B_CUT = 99
    FORCE_FUSED = False
    NO_WG = False
    DBG_OUT = None
    HG_COLS = NT
    D_GROUPS = 8
    E_GROUPS = NT // 256
    def __init__(self, stages, debug=False):
        self.stages = list(stages)
        self.fused = len(stages) == 4 * DEPTH or self.FORCE_FUSED
        self.nc = bass.Bass("TRN2", target_bir_lowering=False)
        self.S = Sched(self.nc)
        self.sb_ptr = SB_BASE
        self.sb_mark = None
        self.uid = 0
        self.bank_rr = 0
        self.dram = {}
        nc = self.nc
        self.ps = nc.alloc_psum_tensor("psum_all", [128, 8 * 512], F32).ap()

    def sb(self, shape, dtype, name=None):
        self.uid += 1
        nbytes = int(np.prod(shape[1:])) * (4 if dtype == F32 else 2)
        off = (self.sb_ptr + 31) // 32 * 32
        assert off + nbytes <= SB_END, f"SBUF overflow {off + nbytes} for {name}"
        self.sb_ptr = off + nbytes
        h = self.nc.alloc_sbuf_tensor_at(f"{name or 't'}_{self.uid}", list(shape), dtype, offset=off)
        return h.ap()

    def bank(self):
        b = self.bank_rr % 8
        self.bank_rr += 1
        return b

    def bank2(self):
        if self.bank_rr % 2:
            self.bank_rr += 1
        b = self.bank_rr % 8
        self.bank_rr += 2
        return b

    def psf(self, b, n=512):
        return self.ps[:, b * 512:b * 512 + n]

    def psb(self, b):
        return self.ps[:, b * 512:(b + 1) * 512].bitcast(BF16)

    def dt(self, name, shape, dtype, kind):
        t = self.nc.dram_tensor(name, list(shape), dtype, kind=kind).ap()
        self.dram[name] = t
        if kind == "ExternalInput":
            self.ext_in.append(name)
        return t

    def has(self, st):
        return st in self.stages

    def xdram(self, name, shape, dtype, producer, consumers):
        here_p = producer is not None and self.has(producer)
        cons_here = any(self.has(c) for c in consumers)
        cons_out = any(not self.has(c) for c in consumers)
        if here_p:
            kind = "ExternalOutput" if cons_out else "Internal"
        elif cons_here:
            kind = "ExternalInput"
        else:
            return None
        return self.dt(name, shape, dtype, kind)

    def mm(self, out, lhsT, rhs, start, stop, reads, writes, **kw):
        self.S.op("pe", lambda E: E.matmul(out, lhsT=lhsT, rhs=rhs, start=start, stop=stop, **kw), reads, writes)

    def tr(self, out, in_, ident, reads, writes):
        self.S.op("pe", lambda E: E.transpose(out, in_, ident), reads, writes)

    def act(self, out, in_, func, reads, writes, **kw):
        self.S.op("act", lambda E: E.activation(out=out, in_=in_, func=func, **kw), reads, writes)

    def dma(self, eng, out, in_, reads, writes, key, **kw):
        self.S.op(eng, lambda E: E.dma_start(out=out, in_=in_, **kw), reads, writes, dma=key)

    def I(self, eng, meth, *args, reads=(), writes=(), **kw):
        self.S.op(eng, lambda E: getattr(E, meth)(*args, **kw), reads, writes)

    def build(self):
        nc, S = self.nc, self.S
        self.ext_in = []
        any_stage = lambda pre: [f"{pre}{l}" for l in range(DEPTH)]
        need_D = any(self.has(s) for s in any_stage("D"))
        need_B = any(self.has(s) for s in any_stage("B"))
        need_E = any(self.has(s) for s in any_stage("E"))
        self.x_in = self.dt("x_own", [NT, D], F32, "ExternalInput") if (self.has("A0") or self.has("D0")) else None
        self.consts = self.dt("consts", [128, 384], F32, "ExternalInput")
        self.gains = self.dt("gains", [DEPTH, 8, D], F32, "ExternalInput")
        if need_B:
            self.w_in = self.dt("w_in_c", [DEPTH, D, 770], F32, "ExternalInput")
            self.bfg = self.dt("bfg_c", [DEPTH, 2, 1], F32, "ExternalInput")
            self.relb = self.dt("relb_c", [DEPTH, 2, 128, 640], F32, "ExternalInput")
            self.cmask = self.dt("cmask", [128, 640], F32, "ExternalInput")
        self.wspec = {}
        if need_D:
            self.mem = self.dt("mem_b", [256, D], F32, "ExternalInput")
            self.wspec.update(w_out=(D, D), w_mem_q=(D, D), w_mem_kv=(D, 2 * D), w_mem_o=(D, D))
        if need_E:
            self.wspec.update(w_gate_up=(D, 2 * DFF), w_down=(DFF, D))
        self.wsh, self.wstage, self.wfull = {}, {}, {}
        for nm, (rows, cols) in self.wspec.items():
            tot = DEPTH * rows
            npc = WPIECES[nm]
            pc = cols // npc
            for q in range(npc):
                key = f"{nm}_{q}"
                self.wsh[key] = self.dt(key + "_sh", [tot // 8, pc], F32, "ExternalInput")
                self.wstage[key] = self.dt(key + "_stg", [tot // 8, pc], F32, "Internal")
                self.wfull[key] = self.dt(key + "_full", [tot, pc], F32, "Internal")
        self.weights_ready = False
        self.hs, self.hg, self.os_, self.og, self.xa, self.xb = {}, {}, {}, {}, {}, {}
        self.ogm = {l: (self.dt(f"ogm{l}", [D, NT], BF16, "Internal") if self.has(f"D{l}") else None) for l in range(DEPTH)}
        for l in range(DEPTH):
            if self.fused:
                self.hs[l] = [self.dt(f"hs{l}_{k}", [D, 512], BF16, "Internal") for k in range(8)]
                self.hg[l] = [self.dt(f"hg{l}_{k}", [4 * D, 512], BF16, "Internal") for k in range(8)]
                self.os_[l] = [self.dt(f"os{l}_{k}", [D, 512], BF16, "Internal") for k in range(8)]
                self.og[l] = [self.dt(f"og{l}_{k}", [4 * D, 512], BF16, "Internal") for k in range(8)]
            else:
                self.hs[l] = [self.xdram(f"hs{l}_{k}", [D, 512], BF16, f"A{l}", ["HOST"]) for k in range(8)]
                self.hg[l] = [self.xdram(f"hg{l}_{k}", [4 * D, 512], BF16, None, [f"B{l}"]) for k in range(8)]
                self.os_[l] = [self.xdram(f"os{l}_{k}", [D, 512], BF16, f"B{l}", ["HOST"]) for k in range(8)]
                self.og[l] = [self.xdram(f"og{l}_{k}", [4 * D, 512], BF16, None, [f"D{l}"]) for k in range(8)]
            self.xa[l] = self.xdram(f"xa{l}", [NT, D], F32, f"D{l}", [f"E{l}"])
            cons = [f"A{l + 1}", f"D{l + 1}"] if l + 1 < DEPTH else ["HOST"]
            self.xb[l] = self.xdram(f"xb{l}" if l + 1 < DEPTH else "y", [NT, D], F32, f"E{l}", cons)

        self.ident = self.sb([128, 128], BF16, "ident")
        self.tri = self.sb([128, 128], BF16, "tri")
        self.identf = self.sb([128, 128], F32, "identf")
        self.e0f = self.sb([128, 128], F32, "e0f")
        self.ones_bf = self.sb([128, 128], BF16, "ones_bf")
        self.ones_f = self.sb([128, 512], F32, "ones_f")
        self.eps_ap = self.sb([128, 1], F32, "eps")
        def get_rank(E):
            r = E.partition_id() % 4
            self.roff = [E.snap(r * 256 + s_ * D) for s_ in range(4)]
            return E.nop()
        S.op("sp", get_rank)
        cst32 = self.sb([128, 384], F32, "cst32")
        self.dma("sp", cst32, self.consts, [], ["cst32"], "cst")
        self.I("dve", "tensor_copy", self.ident, cst32[:, 0:128], reads=["cst32"], writes=["ident"])
        self.I("dve", "tensor_copy", self.tri, cst32[:, 128:256], reads=["cst32"], writes=["tri"])
        self.I("dve", "tensor_copy", self.identf, cst32[:, 0:128], reads=["cst32"], writes=["identf"])
        self.I("dve", "tensor_copy", self.e0f, cst32[:, 256:384], reads=["cst32"], writes=["e0f"])
        self.I("dve", "memset", self.ones_bf, 1.0, writes=["ones_bf"])
        self.I("dve", "memset", self.ones_f, 1.0, writes=["ones_f"])
        self.I("dve", "memset", self.eps_ap, EPS, writes=["eps"])
        self.sb_persist = self.sb_ptr

        for st in self.stages:
            self.sb_ptr = self.sb_persist
            S.barrier()
            l = int(st[1])
            if st[0] == "A":
                self.stage_A(l)
                if self.fused:
                    self.gather_weights()
            elif st[0] == "B":
                self.stage_B(l)
                if self.fused:
                    for k in range(8):
                        self.collective(self.os_[l][k], self.og[l][k], [f"os{l}_{T}" for T in range(self.B_GROUPS) if T % 8 == k],
                                        f"og{l}_{k}", f"cc_o{l}_{k}")
            elif st[0] == "D":
                self.gather_weights()
                self.stage_D(l)
            elif st[0] == "E":
                self.gather_weights()
                self.stage_E(l)
        if self.DBG_OUT:
            for nm in self.DBG_OUT:
                src = self.dram[nm]
                dbg = self.dt("dbg_" + nm, list(src.shape), src.dtype, "ExternalOutput")
                self.dma("sp", dbg, src, [r for r in list(self.S.res) if r.startswith(nm[:3])], ["OUT:dbg_" + nm], "dbg_" + nm)
        outs = [n for n in self.S.res if n.startswith("OUT:")]
        S.op("sp", lambda E: E.nop(), outs, [])
        S.emit()
        return nc

    def collective(self, src, dst, reads, wres, key):
        self.S.op("pool", lambda E: E.collective_compute("AllGather", ALU.bypass, replica_groups=GROUPS,
                                                         ins=[src.opt()], outs=[dst.opt()]),
                  reads, [wres], dma="cc", inc=1)

    def gather_weights(self):
        if self.weights_ready or self.NO_WG:
            return
        self.weights_ready = True
        keys = [f"{nm}_{q}" for nm in self.wspec for q in range(WPIECES[nm])]
        for key in keys:
            self.dma("sp", self.wstage[key], self.wsh[key], [], [key + "_stg"], "wg_s")
        for key in keys:
            self.S.op("pool", lambda E, key=key: E.collective_compute(
                "AllGather", ALU.bypass, replica_groups=[list(range(8))], ins=[self.wstage[key].opt()], outs=[self.wfull[key].opt()]),
                [k_ + "_stg" for k_ in keys], [key + "_full"], dma="wg_c", inc=1)

    def W(self, nm, l, q=0):
        rows = self.wspec[nm][0]
        return self.wfull[f"{nm}_{q}"][l * rows:(l + 1) * rows, :]

    def xsrc(self, l):
        return self.x_in if l == 0 else self.xb[l - 1]

    def xres(self, l, lo, hi):
        if l == 0:
            return ["x_in"]
        return [f"OUT:xb{l - 1}_{g}" for g in range(lo // 256, (hi + 255) // 256)]

    def load_gain_fm(self, l, row, name):
        t = self.sb([128, 8], F32, name)
        self.S.op("sp", lambda E: E.dma_start(out=t, in_=self.gains[l, row, :].rearrange("(c p) -> p c", p=128),
                                              allow_slow_non_contiguous=True), [], [name], dma=name)
        return t

    def load_gain_tm(self, l, row, name):
        t = self.sb([128, D], F32, name)
        self.S.op("sp", lambda E: E.dma_start(out=t, in_=self.gains[l, row:row + 1, :].partition_broadcast(128)),
                  [], [name], dma=name)
        return t

    def norm_transpose(self, tag, x_ap, x_res, hb, hb_res, st, st_res, junk, g_fm, g_res, dstT, dst_cols, dst_res):
        ssq, lnv, rstd = st[:, 0:1], st[:, 1:2], st[:, 2:3]
        self.act(junk, x_ap, AF.Square, list(x_res), [st_res], accum_out=ssq)
        self.act(lnv, ssq, AF.Ln, [st_res], [st_res], scale=1.0 / D, bias=self.eps_ap)
        self.act(rstd, lnv, AF.Exp, [st_res], [st_res], scale=-0.5)
        self.I("dve", "tensor_scalar", hb, x_ap, rstd, None, ALU.mult, reads=list(x_res) + [st_res], writes=[hb_res])
        b = self.bank()
        pb = self.psb(b)
        for c in range(8):
            self.tr(pb[:, c * 128:(c + 1) * 128], hb[:, c * 128:(c + 1) * 128], self.ident, [hb_res, "ident"], [f"ps{b}"])
        src = pb[:, 0:1024].rearrange("p (c t) -> p c t", c=8)
        gb = g_fm.unsqueeze(2).to_broadcast([128, 8, 128])
        self.I("dve", "tensor_tensor", dstT[:, :, dst_cols], src, gb, ALU.mult, reads=[f"ps{b}", g_res], writes=[dst_res])

    def stage_A(self, l):
        xsrc = self.xsrc(l)
        g_pre = self.load_gain_fm(l, 0, f"A{l}_g")
        xt = [self.sb([128, D], F32, "A_x") for _ in range(3)]
        hb = [self.sb([128, D], BF16, "A_h") for _ in range(2)]
        st = [self.sb([128, 4], F32, "A_st") for _ in range(2)]
        junk = self.sb([128, D], BF16, "A_junk")
        hT = [self.sb([128, 8, 512], BF16, "A_hT") for _ in range(2)]
        for g in range(8):
            gs = g % 2
            for tt in range(4):
                t = g * 4 + tt
                xs, hs_ = t % 3, t % 2
                self.dma("sp", xt[xs], xsrc[t * 128:(t + 1) * 128, :], self.xres(l, t * 128, (t + 1) * 128), [f"A_x{xs}"], f"A_x{xs}")
                self.norm_transpose("A", xt[xs], [f"A_x{xs}"], hb[hs_], f"A_h{hs_}", st[hs_], f"A_st{hs_}", junk,
                                    g_pre, f"A{l}_g", hT[gs], slice(tt * 128, (tt + 1) * 128), f"A_hT{gs}")
            dst = self.hs[l][g].rearrange("(c p) t -> p c t", p=128)
            wres = f"hs{l}_{g}" if self.fused else f"OUT:hs{l}_{g}"
            self.dma("pool", dst, hT[gs], [f"A_hT{gs}"], [wres], f"A_hst{gs}")
            if self.fused:
                self.collective(self.hs[l][g], self.hg[l][g], [f"hs{l}_{g}"], f"hg{l}_{g}", f"cc_h{l}_{g}")

    def stage_B(self, l):
        S = self.S
        hg = self.hg[l]
        wB = self.sb([128, 8, 770], BF16, "B_w")
        self.dma("pool", wB, self.w_in[l].rearrange("(c p) n -> p c n", p=128), [], ["B_w"], "B_w")
        negb = self.sb([2, 1], F32, "B_negb")
        self.dma("sp", negb, self.bfg[l], [], ["B_negb"], "B_negb")
        self.I("dve", "tensor_scalar", negb, negb, -1.0, None, ALU.mult, reads=["B_negb"], writes=["B_negb"])
        strip = self.sb([128, 2, 640], F32, "B_strip")
        cm = self.sb([128, 640], F32, "B_cm")
        self.dma("sp", strip, self.relb[l].rearrange("h p n -> p h n"), [], ["B_strip"], "B_strip")
        self.dma("sp", cm, self.cmask, [], ["B_cm"], "B_cm")
        for hh in range(2):
            self.I("dve", "tensor_tensor", strip[:, hh, :], strip[:, hh, :], cm, ALU.add, reads=["B_strip", "B_cm"], writes=["B_strip"])
        KfT = self.sb([128, SEQ], BF16, "B_KfT")
        Vf = self.sb([128, 128, 2, 65], BF16, "B_Vf")
        self.I("pool", "memset", Vf[:, :, :, 64:65], 1.0, writes=["B_Vf_ones"])
        KcT = [self.sb([128, 512], BF16, "B_KcT") for _ in range(2)]
        Vc = [self.sb([128, 4, 2, 65], BF16, "B_Vc") for _ in range(2)]
        for i in range(2):
            self.I("pool", "memset", Vc[i][:, :, :, 64:65], 1.0, writes=[f"B_Vc_ones{i}"])
        negcT = self.sb([128, 128, 2], F32, "B_negcT")
        hTin = [self.sb([128, 8, 512], BF16, "B_hT") for _ in range(2)]
        qfT = [self.sb([128, 512], BF16, "B_qf") for _ in range(2)]
        qcT = [self.sb([128, 512], BF16, "B_qc") for _ in range(2)]
        fe = self.sb([2, 512], F32, "B_fe")
        fsp = self.sb([2, 512], F32, "B_fsp")
        nrow = [self.sb([2, 512], F32, "B_nrow") for _ in range(2)]
        ones2 = self.ones_f[0:2, 0:512]
        refsb = self.sb([128, 2], F32, "B_ref")
        fbias = [self.sb([128, 2, 128], F32, "B_fbias") for _ in range(2)]
        sc = [self.sb([128, 512], F32, "B_sc") for _ in range(2)]
        NP = 4
        pT = [self.sb([128, 512], BF16, "B_pT") for _ in range(NP)]
        rl = self.sb([128, 512], F32, "B_rl")
        bcs = self.sb([64, 512], F32, "B_bcs")
        obuf = [self.sb([64, 4, 512], BF16, "B_obuf") for _ in range(2)]
        cnt = dict(p=0, sc=0, inb=0, sbk=0, acc=0)

        def inb():
            cnt["inb"] += 1
            return [0, 1][cnt["inb"] % 2]

        def sbk():
            cnt["sbk"] += 1
            return [2, 3, 4][cnt["sbk"] % 3]

        def accb():
            cnt["acc"] += 1
            return [5, 6][cnt["acc"] % 2]

        def normalize(acc_b, ob, head, obres):
            acc = self.psf(acc_b)
            self.I("dve", "reciprocal", rl[64:65, :], acc[64:65, :], reads=[f"ps{acc_b}"], writes=["B_rl"])
            bc = self.psf(7)
            self.mm(bc[0:64, :], self.ones_f[64:65, 0:64], rl[64:65, :], True, True, ["B_rl", "ones_f"], ["ps7"])
            self.act(bcs, bc[0:64, :], AF.Copy, ["ps7"], ["B_bcs"])
            self.I("dve", "tensor_tensor", ob[:, head, :], acc[0:64, :], bcs, ALU.mult,
                   reads=[f"ps{acc_b}", "B_bcs"], writes=[obres])

        def load_group(T):
            cur = T % 2
            s_rank, k = T // 8, T % 8
            src = hg[k][s_rank * D:(s_rank + 1) * D, :].rearrange("(c p) t -> p c t", p=128)
            self.dma("sp", hTin[cur], src, [f"hg{l}_{k}"], [f"B_hT{cur}"], f"B_hT{cur}")

        load_group(0)
        for T in range(self.B_GROUPS):
            cur = T % 2
            prv = 1 - cur
            gcol = (T % 2) * 512
            if T + 1 < self.B_GROUPS:
                load_group(T + 1)
            hres = f"B_hT{cur}"
            for oc in range(4):
                b = inb()
                for kc in range(8):
                    self.mm(self.psf(b), wB[:, kc, oc * 128:(oc + 1) * 128], hTin[cur][:, kc, :], kc == 0, kc == 7,
                            ["B_w", hres], [f"ps{b}"])
                if oc == 0:
                    self.act(qfT[cur], self.psf(b), AF.Copy, [f"ps{b}"], [f"B_qf{cur}"])
                elif oc == 1:
                    self.I("dve", "tensor_copy", KfT[:, T * 512:(T + 1) * 512], self.psf(b), reads=[f"ps{b}"], writes=[f"B_KfT_{T}"])
                elif oc == 2:
                    self.act(qcT[cur], self.psf(b), AF.Copy, [f"ps{b}"], [f"B_qc{cur}"])
                else:
                    self.I("dve", "tensor_copy", KcT[cur], self.psf(b), reads=[f"ps{b}"], writes=[f"B_KcT{cur}"])
            if self.B_CUT < 3:
                continue
            b = inb()
            for kc in range(8):
                self.mm(self.psf(b)[0:2, :], wB[:, kc, 768:770], hTin[cur][:, kc, :], kc == 0, kc == 7, ["B_w", hres], [f"ps{b}"])
            self.act(fe, self.psf(b)[0:2, :], AF.Exp, [f"ps{b}", "B_negb"], ["B_fe"], scale=-1.0, bias=negb)
            self.act(fsp, fe, AF.Ln, ["B_fe", "ones_f"], ["B_fsp"], bias=self.ones_f[0:2, 0:1])
            init = 0.0 if T == 0 else nrow[prv][:, 511:512]
            self.I("dve", "tensor_tensor_scan", nrow[cur], ones2, fsp, init, ALU.mult, ALU.add,
                   reads=["B_fsp", "ones_f", f"B_nrow{prv}"], writes=[f"B_nrow{cur}"])
            b = inb()
            for j in range(4):
                self.tr(self.psf(b)[:, 2 * j:2 * j + 2], nrow[cur][:, j * 128:(j + 1) * 128], self.identf[0:2, 0:2],
                        [f"B_nrow{cur}", "identf"], [f"ps{b}"])
            self.I("dve", "tensor_copy", negcT[:, 4 * T:4 * T + 4, :], self.psf(b)[:, 0:8].rearrange("p (j h) -> p j h", h=2),
                   reads=[f"ps{b}"], writes=["B_negcT"])
            b = inb()
            self.mm(self.psf(b)[:, 0:2], self.e0f, negcT[:, 4 * T + 2, :], True, True, ["B_negcT", "e0f"], [f"ps{b}"])
            self.I("dve", "tensor_copy", refsb, self.psf(b)[:, 0:2], reads=[f"ps{b}"], writes=["B_ref"])
            nkb = 4 * T + 4
            for hh in range(2):
                self.I("dve", "tensor_scalar", fbias[cur][:, hh, 0:nkb], negcT[:, 0:nkb, hh], refsb[:, hh:hh + 1], None, ALU.subtract,
                       reads=["B_negcT", "B_ref"], writes=[f"B_fbias{cur}"])
            if self.B_CUT < 4:
                continue
            for tt in range(4):
                b = inb()
                for kc in range(8):
                    self.mm(self.psf(b)[:, 0:256], hTin[cur][:, kc, tt * 128:(tt + 1) * 128], wB[:, kc, 512:768],
                            kc == 0, kc == 7, ["B_w", hres], [f"ps{b}"])
                blk = 4 * T + tt
                import os
                if not os.environ.get("SKIPVF"):
                    self.I("dve", "tensor_copy", Vf[:, blk, :, 0:64], self.psf(b)[:, 0:128].rearrange("p (h d) -> p h d", h=2),
                           reads=[f"ps{b}", "B_Vf_ones"], writes=[f"B_Vf_{T}"])
                if not os.environ.get("SKIPVC"):
                    self.act(Vc[cur][:, tt, :, 0:64], self.psf(b)[:, 128:256].rearrange("p (h d) -> p h d", h=2), AF.Copy,
                             [f"ps{b}", f"B_Vc_ones{cur}"], [f"B_Vc{cur}"])
            ob = obuf[cur]
            obres = f"B_obuf{cur}"
            if self.B_CUT < 5:
                continue
            for hh in range(2):
                hs_ = slice(hh * 64, (hh + 1) * 64)
                acc_b = accb()
                kbs = [kb for kb in range(8) if not (T == 0 and kb < 4)]
                info = {}

                def c_qk(kb, hh=hh, hs_=hs_):
                    i_lo, i_hi = max(0, kb - 4), min(3, kb)
                    c0, ncol = i_lo * 128, (i_hi - i_lo + 1) * 128
                    slot, blk = (prv, kb) if kb < 4 else (cur, kb - 4)
                    off = (4 + i_lo - kb) * 128
                    sb_ = sbk()
                    self.mm(self.psf(sb_)[:, 0:ncol], KcT[slot][hs_, blk * 128:(blk + 1) * 128], qcT[cur][hs_, c0:c0 + ncol],
                            True, True, [f"B_KcT{slot}", f"B_qc{cur}"], [f"ps{sb_}"])
                    info[kb] = (c0, ncol, slot, blk, off, sb_)

                def c_ex(kb, hh=hh):
                    c0, ncol, slot, blk, off, sb_ = info[kb]
                    si = cnt["sc"] % 2
                    cnt["sc"] += 1
                    self.I("dve", "scalar_tensor_tensor", sc[si][:, 0:ncol], self.psf(sb_)[:, 0:ncol], 0.125,
                           strip[:, hh, off:off + ncol], ALU.mult, ALU.add, reads=[f"ps{sb_}", "B_strip"], writes=[f"B_sc{si}"])
                    pi = cnt["p"] % NP
                    cnt["p"] += 1
                    self.act(pT[pi][:, 0:ncol], sc[si][:, 0:ncol], AF.Exp, [f"B_sc{si}"], [f"B_pT{pi}"])
                    info[kb] = info[kb] + (pi,)

                def c_pv(kb, hh=hh, acc_b=acc_b):
                    c0, ncol, slot, blk, off, sb_, pi = info[kb]
                    self.mm(self.psf(acc_b)[0:65, c0:c0 + ncol], Vc[slot][:, blk, hh, :], pT[pi][:, 0:ncol], kb == kbs[0], kb == kbs[-1],
                            [f"B_Vc{slot}", f"B_Vc_ones{slot}", f"B_pT{pi}"], [f"ps{acc_b}"], skip_group_check=True)

                LA = 2
                for q_ in range(min(LA, len(kbs))):
                    c_qk(kbs[q_])
                for q_, kb in enumerate(kbs):
                    c_ex(kb)
                    if q_ + LA < len(kbs):
                        c_qk(kbs[q_ + LA])
                    c_pv(kb)
                normalize(acc_b, ob, 2 + hh, obres)
            if self.B_CUT < 6:
                continue
            for hh in range(2):
                hs_ = slice(hh * 64, (hh + 1) * 64)
                acc_b = accb()
                nj = 4 * T + 4
                finfo = {}

                def f_qk(j, hs_=hs_):
                    diag = j >= 4 * T
                    c0 = (j - 4 * T) * 128 if diag else 0
                    ncol = 512 - c0
                    sb_ = sbk()
                    self.mm(self.psf(sb_)[:, 0:ncol], KfT[hs_, j * 128:(j + 1) * 128], qfT[cur][hs_, c0:512], True, True,
                            [f"B_KfT_{j // 4}", f"B_qf{cur}"], [f"ps{sb_}"])
                    finfo[j] = (diag, c0, ncol, sb_)

                def f_ex(j, hh=hh):
                    diag, c0, ncol, sb_ = finfo[j]
                    pi = cnt["p"] % NP
                    cnt["p"] += 1
                    self.act(pT[pi][:, 0:ncol], self.psf(sb_)[:, 0:ncol], AF.Exp, [f"ps{sb_}", f"B_fbias{cur}"], [f"B_pT{pi}"],
                             scale=0.125, bias=fbias[cur][:, hh, j:j + 1])
                    if diag:
                        self.I("pool", "tensor_tensor", pT[pi][:, 0:128], pT[pi][:, 0:128], self.tri, ALU.mult,
                               reads=[f"B_pT{pi}", "tri"], writes=[f"B_pT{pi}"])
                    finfo[j] = finfo[j] + (pi,)

                def f_pv(j, hh=hh, acc_b=acc_b):
                    diag, c0, ncol, sb_, pi = finfo[j]
                    self.mm(self.psf(acc_b)[0:65, c0:512], Vf[:, j, hh, :], pT[pi][:, 0:ncol], j == 0, j == nj - 1,
                            [f"B_Vf_{j // 4}", "B_Vf_ones", f"B_pT{pi}"], [f"ps{acc_b}"], skip_group_check=True)

                LA = 2
                for j in range(min(LA, nj)):
                    f_qk(j)
                for j in range(nj):
                    f_ex(j)
                    if j + LA < nj:
                        f_qk(j + LA)
                    f_pv(j)
                normalize(acc_b, ob, hh, obres)
            i_rng = T // 8
            dst = self.os_[l][T % 8][i_rng * 256:(i_rng + 1) * 256, :].rearrange("(h d) t -> d h t", d=64)
            wres = f"os{l}_{T}" if self.fused else f"OUT:os{l}_{T}"
            self.dma("pool", dst, ob, [obres], [wres], f"B_ost{cur}")

    def stage_D(self, l):
        S = self.S
        og = self.og[l]
        xsrc = self.xsrc(l)
        Wout = self.sb([128, 8, D], BF16, "D_Wout")
        Wq = self.sb([128, 8, D], BF16, "D_Wq")
        Wo = self.sb([128, 8, D], BF16, "D_Wo")
        for w, src, nm in ((Wout, "w_out", "D_Wout"), (Wq, "w_mem_q", "D_Wq"), (Wo, "w_mem_o", "D_Wo")):
            self.dma("pool", w, self.W(src, l).rearrange("(c p) n -> p c n", p=128), [src + "_0_full"], [nm], nm)
        g_cat = self.load_gain_fm(l, 1, "D_gcat")
        for c in range(8):
            self.I("dve", "tensor_scalar", Wout[:, c, :], Wout[:, c, :], g_cat[:, c:c + 1], None, ALU.mult,
                   reads=["D_Wout", "D_gcat"], writes=["D_Wout"])
        g_mempre = self.load_gain_fm(l, 3, "D_gmempre")
        g_memkv = self.load_gain_fm(l, 4, "D_gmemkv")
        g_mixpost = self.load_gain_tm(l, 2, "D_gmixpost")
        g_mempost = self.load_gain_tm(l, 5, "D_gmempost")
        KmT = self.sb([128, 8, 256], BF16, "D_KmT")
        Vm = self.sb([128, 2, D], BF16, "D_Vm")
        mark = self.sb_ptr
        Wkv = self.sb([128, 8, 2 * D], BF16, "D_Wkv")
        for q in range(2):
            self.dma("pool", Wkv[:, :, q * D:(q + 1) * D], self.W("w_mem_kv", l, q).rearrange("(c p) n -> p c n", p=128),
                     [f"w_mem_kv_{q}_full"], ["D_Wkv"], "D_Wkv")
        memt = self.sb([128, 2, D], F32, "D_memt")
        self.dma("sp", memt, self.mem.rearrange("(j p) f -> p j f", p=128), [], ["D_memt"], "D_memt")
        mT = self.sb([128, 8, 256], BF16, "D_mT")
        mhb = self.sb([128, D], BF16, "D_mhb")
        mst = self.sb([128, 4], F32, "D_mst")
        mjunk = self.sb([128, D], BF16, "D_mjunk")
        for j in range(2):
            self.norm_transpose("Dm", memt[:, j, :], ["D_memt"], mhb, "D_mhb", mst, "D_mst", mjunk, g_memkv, "D_gmemkv",
                                mT, slice(j * 128, (j + 1) * 128), "D_mT")
        for fc in range(8):
            b = self.bank()
            for kc in range(8):
                self.mm(self.psf(b)[:, 0:256], Wkv[:, kc, fc * 128:(fc + 1) * 128], mT[:, kc, :], kc == 0, kc == 7,
                        ["D_Wkv", "D_mT"], [f"ps{b}"])
            self.I("dve", "tensor_copy", KmT[:, fc, :], self.psf(b)[:, 0:256], reads=[f"ps{b}"], writes=["D_KmT"])
        for mb in range(2):
            for n in range(2):
                b = self.bank()
                for kc in range(8):
                    self.mm(self.psf(b), mT[:, kc, mb * 128:(mb + 1) * 128], Wkv[:, kc, D + n * 512:D + (n + 1) * 512],
                            kc == 0, kc == 7, ["D_Wkv", "D_mT"], [f"ps{b}"])
                self.act(Vm[:, mb, n * 512:(n + 1) * 512], self.psf(b), AF.Copy, [f"ps{b}"], ["D_Vm"])
        self.sb_ptr = mark
        S.barrier()
        tmp_res = []
        oT = [self.sb([128, 8, 512], BF16, "D_oT") for _ in range(2)]
        sq = self.sb([128, 8, 512], BF16, "D_sq")
        xg = [self.sb([128, 4, D], F32, "D_xg") for _ in range(2)]
        st8 = self.sb([128, 8], F32, "D_st8")
        rstd_o = self.sb([128, 8], F32, "D_rstdo")
        mix = self.sb([128, D], F32, "D_mix")
        stt = self.sb([128, 4], F32, "D_stt")
        junk = self.sb([128, D], BF16, "D_junk")
        hb = [self.sb([128, D], BF16, "D_hb") for _ in range(2)]
        stn = [self.sb([128, 4], F32, "D_stn") for _ in range(2)]
        h2T = self.sb([128, 8, 512], BF16, "D_h2T")
        qT = self.sb([128, 8, 512], BF16, "D_qT")
        pTm = [self.sb([128, 2, 512], BF16, "D_pTm") for _ in range(2)]
        rlm = self.sb([128, 512], F32, "D_rlm")
        OmT = self.sb([128, 8, 512], BF16, "D_OmT")
        tmp = self.sb([128, D], F32, "D_tmp")
        alias_guard = tmp_res

        def post_norm_add(src_ap, src_res, g_tm, g_res, xdst, xres_):
            ssq, lnv, rs = stt[:, 0:1], stt[:, 1:2], stt[:, 2:3]
            self.act(junk, src_ap, AF.Square, src_res, ["D_stt"], accum_out=ssq)
            self.act(lnv, ssq, AF.Ln, ["D_stt"], ["D_stt"], scale=1.0 / D, bias=self.eps_ap)
            self.act(rs, lnv, AF.Exp, ["D_stt"], ["D_stt"], scale=-0.5)
            self.I("dve", "scalar_tensor_tensor", tmp, src_ap, rs, g_tm, ALU.mult, ALU.mult,
                   reads=src_res + ["D_stt", g_res], writes=["D_tmp"])
            self.I("pool", "tensor_tensor", xdst, xdst, tmp, ALU.add, reads=["D_tmp", xres_], writes=[xres_])

        ogm = self.ogm[l]

        def mk_cp(s_, k):
            def cp(E):
                return E.dma_start(out=ogm[s_ * 256:(s_ + 1) * 256, k * 512:(k + 1) * 512], in_=og[k][bass.ds(self.roff[s_], 256), :])
            return cp
        for k in range(8):
            for s_ in range(4):
                S.op("sp", mk_cp(s_, k), [f"og{l}_{k}"], [f"ogm{l}_{k}_{s_}"], dma=f"D_ogm{l}")
        ogm_all = [f"ogm{l}_{k}_{s_}" for k in range(8) for s_ in range(4)]

        for g in range(self.D_GROUPS):
            cur = g % 2
            xgres = f"D_xg{cur}"
            ores = [f"D_oT{cur}_{kk}" for kk in range(2)]
            for half in range(2):
                src = ogm[:, g * 512:(g + 1) * 512].rearrange("(s h p) t -> p h s t", h=2, p=128)[:, half]
                self.dma("sp", oT[cur][:, half * 4:(half + 1) * 4, :], src, ogm_all, [ores[half]], f"D_oT{cur}")
            self.dma("sp", xg[cur], xsrc[g * 512:(g + 1) * 512, :].rearrange("(t p) f -> p t f", p=128),
                     self.xres(l, g * 512, (g + 1) * 512), [xgres] + (alias_guard if g < 2 else []), xgres)
            self.I("pool", "tensor_tensor", sq, oT[cur], oT[cur], ALU.mult, reads=ores, writes=["D_sq"] + (alias_guard if g == 0 else []))
            b = self.bank()
            for tt in range(4):
                for grp in range(2):
                    col = tt * 2 + grp
                    for s in range(4):
                        self.mm(self.psf(b)[:, col:col + 1], sq[:, grp * 4 + s, tt * 128:(tt + 1) * 128], self.ones_bf[:, 0:1],
                                s == 0, s == 3, ["D_sq", "ones_bf"], [f"ps{b}"])
            self.act(st8, self.psf(b)[:, 0:8], AF.Ln, [f"ps{b}"], ["D_st8"] + (alias_guard if g == 0 else []), scale=1.0 / 512, bias=self.eps_ap)
            self.act(rstd_o, st8, AF.Exp, ["D_st8"], ["D_rstdo"], scale=-0.5)
            for tt in range(4):
                tc = slice(tt * 128, (tt + 1) * 128)
                bF = self.bank2()
                for n in range(2):
                    for s in range(4):
                        self.mm(self.psf(bF + n), oT[cur][:, s, tc], Wout[:, s, n * 512:(n + 1) * 512], s == 0, s == 3,
                                ores + ["D_Wout"], [f"ps{bF + n}"])
                bC = self.bank2()
                for n in range(2):
                    for s in range(4):
                        self.mm(self.psf(bC + n), oT[cur][:, 4 + s, tc], Wout[:, 4 + s, n * 512:(n + 1) * 512], s == 0, s == 3,
                                ores + ["D_Wout"], [f"ps{bC + n}"])
                pF = self.ps[:, bF * 512:bF * 512 + 1024]
                pC = self.ps[:, bC * 512:bC * 512 + 1024]
                self.I("dve", "tensor_scalar", mix, pF, rstd_o[:, 2 * tt:2 * tt + 1], None, ALU.mult,
                       reads=[f"ps{bF}", f"ps{bF + 1}", "D_rstdo"], writes=["D_mix"])
                self.I("dve", "scalar_tensor_tensor", mix, pC, rstd_o[:, 2 * tt + 1:2 * tt + 2], mix, ALU.mult, ALU.add,
                       reads=[f"ps{bC}", f"ps{bC + 1}", "D_rstdo"], writes=["D_mix"])
                post_norm_add(mix, ["D_mix"], g_mixpost, "D_gmixpost", xg[cur][:, tt, :], xgres)
                hi = tt % 2
                self.norm_transpose("D", xg[cur][:, tt, :], [xgres], hb[hi], f"D_hb{hi}", stn[hi], f"D_stn{hi}", junk,
                                    g_mempre, "D_gmempre", h2T, tc, "D_h2T")
            for oc in range(8):
                b = self.bank()
                for kc in range(8):
                    self.mm(self.psf(b), Wq[:, kc, oc * 128:(oc + 1) * 128], h2T[:, kc, :], kc == 0, kc == 7,
                            ["D_Wq", "D_h2T"], [f"ps{b}"])
                if oc % 2 == 0:
                    self.act(qT[:, oc, :], self.psf(b), AF.Copy, [f"ps{b}"], ["D_qT"])
                else:
                    self.I("dve", "tensor_copy", qT[:, oc, :], self.psf(b), reads=[f"ps{b}"], writes=["D_qT"])
            for hm in range(4):
                pm = pTm[hm % 2]
                pres = f"D_pTm{hm % 2}"
                for mb in range(2):
                    b = self.bank()
                    for dh in range(2):
                        self.mm(self.psf(b), KmT[:, 2 * hm + dh, mb * 128:(mb + 1) * 128], qT[:, 2 * hm + dh, :], dh == 0, dh == 1,
                                ["D_KmT", "D_qT"], [f"ps{b}"])
                    self.act(pm[:, mb, :], self.psf(b), AF.Exp, [f"ps{b}"], [pres], scale=1.0 / 16.0)
                b = self.bank()
                for mb in range(2):
                    self.mm(self.psf(b), self.ones_bf, pm[:, mb, :], mb == 0, mb == 1, [pres, "ones_bf"], [f"ps{b}"])
                self.I("dve", "reciprocal", rlm, self.psf(b), reads=[f"ps{b}"], writes=["D_rlm"])
                for dh in range(2):
                    b = self.bank()
                    for mb in range(2):
                        self.mm(self.psf(b), Vm[:, mb, (2 * hm + dh) * 128:(2 * hm + dh + 1) * 128], pm[:, mb, :], mb == 0, mb == 1,
                                ["D_Vm", pres], [f"ps{b}"])
                    self.I("dve", "tensor_tensor", OmT[:, 2 * hm + dh, :], self.psf(b), rlm, ALU.mult,
                           reads=[f"ps{b}", "D_rlm"], writes=["D_OmT"])
            for tt in range(4):
                tc = slice(tt * 128, (tt + 1) * 128)
                bO = self.bank2()
                for n in range(2):
                    for kk in range(8):
                        self.mm(self.psf(bO + n), OmT[:, kk, tc], Wo[:, kk, n * 512:(n + 1) * 512], kk == 0, kk == 7,
                                ["D_OmT", "D_Wo"], [f"ps{bO + n}"])
                pO = self.ps[:, bO * 512:bO * 512 + 1024]
                post_norm_add(pO, [f"ps{bO}", f"ps{bO + 1}"], g_mempost, "D_gmempost", xg[cur][:, tt, :], xgres)
            self.dma("pool", self.xa[l][g * 512:(g + 1) * 512, :].rearrange("(t p) f -> p t f", p=128), xg[cur],
                     [xgres], [f"OUT:xa{l}_{g}"], f"D_xst{cur}")

    def stage_E(self, l):
        Wgu = self.sb([128, 8, 2 * DFF], BF16, "E_Wgu")
        Wdn = self.sb([128, NFC, D], BF16, "E_Wdn")
        pcw = 2 * DFF // 4
        for q in range(4):
            self.dma("pool", Wgu[:, :, q * pcw:(q + 1) * pcw], self.W("w_gate_up", l, q).rearrange("(c p) n -> p c n", p=128),
                     [f"w_gate_up_{q}_full"], ["E_Wgu"], "E_Wgu")
        for q in range(2):
            self.dma("pool", Wdn[:, :, q * 512:(q + 1) * 512], self.W("w_down", l, q).rearrange("(c p) n -> p c n", p=128),
                     [f"w_down_{q}_full"], ["E_Wdn"], "E_Wdn")
        g_pre = self.load_gain_fm(l, 6, "E_gpre")
        g_post = self.load_gain_tm(l, 7, "E_gpost")
        NG = 256
        xg = [self.sb([128, 2, D], F32, "E_xg") for _ in range(2)]
        hb = [self.sb([128, D], BF16, "E_hb") for _ in range(2)]
        stn = [self.sb([128, 4], F32, "E_stn") for _ in range(2)]
        junk = self.sb([128, D], BF16, "E_junk")
        h3T = self.sb([128, 8, NG], BF16, "E_h3T")
        actT = self.sb([128, NFC, NG], BF16, "E_actT")
        sg = [self.sb([128, NG], F32, "E_sg") for _ in range(2)]
        stt = self.sb([128, 4], F32, "E_stt")
        tmp = self.sb([128, D], F32, "E_tmp")
        is_last = l + 1 == DEPTH
        for g in range(self.E_GROUPS):
            cur = g % 2
            xgres = f"E_xg{cur}"
            self.dma("sp", xg[cur], self.xa[l][g * NG:(g + 1) * NG, :].rearrange("(t p) f -> p t f", p=128),
                     [f"OUT:xa{l}_{(g * NG) // 512}"], [xgres], xgres)
            for tt in range(2):
                hi = tt % 2
                self.norm_transpose("E", xg[cur][:, tt, :], [xgres], hb[hi], f"E_hb{hi}", stn[hi], f"E_stn{hi}", junk,
                                    g_pre, "E_gpre", h3T, slice(tt * 128, (tt + 1) * 128), "E_h3T")
            for fc in range(NFC):
                bg = self.bank()
                for kc in range(8):
                    self.mm(self.psf(bg)[:, 0:NG], Wgu[:, kc, fc * 128:(fc + 1) * 128], h3T[:, kc, :], kc == 0, kc == 7,
                            ["E_Wgu", "E_h3T"], [f"ps{bg}"])
                bu = self.bank()
                for kc in range(8):
                    self.mm(self.psf(bu)[:, 0:NG], Wgu[:, kc, DFF + fc * 128:DFF + (fc + 1) * 128], h3T[:, kc, :], kc == 0, kc == 7,
                            ["E_Wgu", "E_h3T"], [f"ps{bu}"])
                si = fc % 2
                self.act(sg[si], self.psf(bg)[:, 0:NG], AF.Silu, [f"ps{bg}"], [f"E_sg{si}"])
                self.I("dve", "tensor_tensor", actT[:, fc, :], self.psf(bu)[:, 0:NG], sg[si], ALU.mult,
                       reads=[f"ps{bu}", f"E_sg{si}"], writes=["E_actT"])
            for tt in range(2):
                tc = slice(tt * 128, (tt + 1) * 128)
                bO = self.bank2()
                for n in range(2):
                    for fc in range(NFC):
                        self.mm(self.psf(bO + n), actT[:, fc, tc], Wdn[:, fc, n * 512:(n + 1) * 512], fc == 0, fc == NFC - 1,
                                ["E_actT", "E_Wdn"], [f"ps{bO + n}"])
                pO = self.ps[:, bO * 512:bO * 512 + 1024]
                ssq, lnv, rs = stt[:, 0:1], stt[:, 1:2], stt[:, 2:3]
                self.act(junk, pO, AF.Square, [f"ps{bO}", f"ps{bO + 1}"], ["E_stt"], accum_out=ssq)
                self.act(lnv, ssq, AF.Ln, ["E_stt"], ["E_stt"], scale=1.0 / D, bias=self.eps_ap)
                self.act(rs, lnv, AF.Exp, ["E_stt"], ["E_stt"], scale=-0.5)
                self.I("dve", "scalar_tensor_tensor", tmp, pO, rs, g_post, ALU.mult, ALU.mult,
                       reads=[f"ps{bO}", f"ps{bO + 1}", "E_stt", "E_gpost"], writes=["E_tmp"])
                self.I("pool", "tensor_tensor", xg[cur][:, tt, :], xg[cur][:, tt, :], tmp, ALU.add, reads=["E_tmp", xgres], writes=[xgres])
            wname = f"OUT:y_{g}" if is_last else f"OUT:xb{l}_{g}"
            self.dma("pool", self.xb[l][g * NG:(g + 1) * NG, :].rearrange("(t p) f -> p t f", p=128), xg[cur],
                     [xgres], [wname], f"E_xst{cur}")


_PROG_CACHE = {}


def _program(stages):
    key = tuple(stages)
    if key not in _PROG_CACHE:
        b = Builder(stages)
        nc = b.build()
        _PROG_CACHE[key] = (nc, list(b.ext_in))
    return _PROG_CACHE[key]


def _consts():
    c = np.zeros((128, 384), np.float32)
    c[:, 0:128] = np.eye(128, dtype=np.float32)
    k = np.arange(128)[:, None]
    q = np.arange(128)[None, :]
    c[:, 128:256] = (k <= q).astype(np.float32)
    c[0, 256:384] = 1.0
    return c


def _chunk_tables(rel_bias):
    k = np.arange(128)[:, None]
    q = np.arange(128)[None, :]
    strips = np.zeros((DEPTH, 8, 128, 640), np.float32)
    mask = np.zeros((128, 640), np.float32)
    for delta in range(5):
        dist = (q - k) + 128 * delta
        idx = np.clip(dist, -63, 128) + 63
        col = delta * 128
        strips[:, :, :, col:col + 128] = rel_bias[:, :, idx]
        if delta == 0:
            mask[64:128, col:col + 64] = NEG
        if delta == 4:
            mask[0:64, col + 64:col + 128] = NEG
    return strips, mask


def _core_inputs(inp, c):
    b, r = c // 4, c % 4
    f = lambda a: np.ascontiguousarray(np.asarray(a, dtype=np.float32))
    w_in = f(inp["w_in"])
    cols = np.concatenate([
        np.arange(r * 128, (r + 1) * 128),
        512 + np.arange(r * 128, (r + 1) * 128),
        1544 + np.arange(r * 128, (r + 1) * 128),
        1544 + 512 + np.arange(r * 128, (r + 1) * 128),
        1024 + np.arange(r * 128, (r + 1) * 128),
        1544 + 1024 + np.arange(r * 128, (r + 1) * 128),
        1536 + np.arange(2 * r, 2 * r + 2),
    ])
    strips, mask = _chunk_tables(f(inp["rel_bias"]))
    gains = np.stack([
        f(inp["g_mix_pre"]), np.concatenate([f(inp["g_fox_out"]), f(inp["g_chunk_out"])], axis=1), f(inp["g_mix_post"]),
        f(inp["g_mem_pre"]), f(inp["g_mem_kv"]), f(inp["g_mem_post"]), f(inp["g_ffn_pre"]), f(inp["g_ffn_post"]),
    ], axis=1)
    return {
        "x_own": f(inp["x"][b, r * NT:(r + 1) * NT, :]),
        "consts": _consts(),
        "gains": np.ascontiguousarray(gains),
        "w_in_c": np.ascontiguousarray(w_in[:, :, cols]),
        "bfg_c": np.ascontiguousarray(f(inp["b_fgate"])[:, 2 * r:2 * r + 2, None]),
        "relb_c": np.ascontiguousarray(strips[:, 2 * r:2 * r + 2]),
        "cmask": mask,
        "mem_b": f(inp["mem"][b]),
        **{f"{nm}_{q}_sh": _shard(f(inp[nm]), c, q, npc) for nm, npc in WPIECES.items() for q in range(npc)},
    }


def _shard(w, c, q, npc):
    w2 = w.reshape(-1, w.shape[-1])
    n = w2.shape[0] // 8
    pc = w2.shape[1] // npc
    return np.ascontiguousarray(w2[c * n:(c + 1) * n, q * pc:(q + 1) * pc])


def _run(stages, maps):
    nc, names = _program(stages)
    in_maps = [{k: m[k] for k in names} for m in maps]
    res = run_bass_kernel_spmd(nc, in_maps, core_ids=list(range(8)))
    return res.results


def _gather(outs, name):
    res = []
    for c in range(8):
        g = GROUPS[c // 4]
        res.append(np.concatenate([np.asarray(outs[i][name]) for i in g], axis=0))
    return res


FUSED = True
SPLIT_LAUNCHES = [["A0"], ["B0"], ["D0", "E0", "A1"], ["B1"], ["D1", "E1"]]
ALL_STAGES = ["A0", "B0", "D0", "E0", "A1", "B1", "D1", "E1"]


def kernel(**inputs):
    maps = [_core_inputs(inputs, c) for c in range(8)]
    if FUSED:
        outs = _run(ALL_STAGES, maps)
    else:
        outs = None
        for stages in SPLIT_LAUNCHES:
            outs = _run(stages, maps)
            for c in range(8):
                for k, v in outs[c].items():
                    maps[c][k] = np.asarray(v)
            for l in range(DEPTH):
                if f"A{l}" in stages:
                    gl = _gather(outs, f"hs{l}")
                    for c in range(8):
                        maps[c][f"hg{l}"] = gl[c]
                if f"B{l}" in stages:
                    gl = _gather(outs, f"os{l}")
                    for c in range(8):
                        maps[c][f"og{l}"] = gl[c]
    y = np.empty((2, SEQ, D), np.float32)
    for c in range(8):
        b, r = c // 4, c % 4
        y[b, r * NT:(r + 1) * NT, :] = np.asarray(outs[c]["y"])
    return y
```
